# Optimizing a Trainium2 kernel written in Bass

```python
import math
import jax, jax.numpy as jnp
from jax import lax
import numpy as np

D_MODEL = 2048
BATCH = 4
SEQ = 4096
DEPTH = 1

N_HEADS = 8
HEAD_DIM = 64
ATTN_W = N_HEADS * 2 * HEAD_DIM
Q_BLOCK = 128
CONV_CH = 1024
CONV_WIDTH = 31
NUM_BUCKETS = 32
MAX_DISTANCE = 128
D_FF = 5632
N_IN = 3 * ATTN_W + 2 * CONV_CH + 2 * D_MODEL
NEG_INF = -1e30

kernel_name = "hybrid_diffattn_conformer_conv_gated"


def rms_norm(x, g, eps=1e-6):
    xf = x.astype(jnp.float32)
    y = xf * lax.rsqrt(jnp.mean(xf * xf, axis=-1, keepdims=True) + eps)
    return (y * g.astype(jnp.float32)).astype(x.dtype)


def layer_norm(x, g, b, eps=1e-5):
    xf = x.astype(jnp.float32)
    mu = jnp.mean(xf, axis=-1, keepdims=True)
    var = jnp.mean(jnp.square(xf - mu), axis=-1, keepdims=True)
    y = (xf - mu) * lax.rsqrt(var + eps)
    return (y * g.astype(jnp.float32) + b.astype(jnp.float32)).astype(x.dtype)


def swiglu(h, w_gate, w_up, w_down):
    return (jax.nn.silu(h @ w_gate) * (h @ w_up)) @ w_down


def t5_bucket(n):
    max_exact = NUM_BUCKETS // 2
    nf = jnp.maximum(n, 1).astype(jnp.float32)
    large = max_exact + (jnp.log(nf / max_exact) / math.log(MAX_DISTANCE / max_exact)
                         * (NUM_BUCKETS - max_exact)).astype(jnp.int32)
    large = jnp.minimum(large, NUM_BUCKETS - 1)
    return jnp.where(n < max_exact, n, large)


def diff_attention(q1, q2, k1, k2, v, lam, rel_bias_table):
    B, H, S, dh = q1.shape
    dv = v.shape[-1]
    nb = S // Q_BLOCK
    scale = dh ** -0.5
    kpos = jnp.arange(S, dtype=jnp.int32)

    def to_blocks(t):
        return t.reshape(B, H, nb, Q_BLOCK, dh).transpose(2, 0, 1, 3, 4)

    def one_block(args):
        qb1, qb2, start = args
        qpos = start + jnp.arange(Q_BLOCK, dtype=jnp.int32)
        rel = qpos[:, None] - kpos[None, :]
        causal = rel >= 0
        bias = jnp.transpose(rel_bias_table[t5_bucket(jnp.maximum(rel, 0))],
                             (2, 0, 1)).astype(jnp.float32)

        def probs(qb, k):
            s = jnp.einsum('bhqd,bhkd->bhqk', qb, k).astype(jnp.float32) * scale + bias
            s = jnp.where(causal, s, NEG_INF)
            return jax.nn.softmax(s, axis=-1)

        a = probs(qb1, k1) - lam * probs(qb2, k2)
        return jnp.einsum('bhqk,bhkd->bhqd', a.astype(v.dtype), v)

    starts = jnp.arange(nb, dtype=jnp.int32) * Q_BLOCK
    out = lax.map(one_block, (to_blocks(q1), to_blocks(q2), starts))
    return out.transpose(1, 0, 3, 2, 4).reshape(B, S, H, dv)


def causal_depthwise_conv(u, w, b):
    C = u.shape[-1]
    y = lax.conv_general_dilated(
        u, w[:, None, :].astype(u.dtype), window_strides=(1,),
        padding=[(CONV_WIDTH - 1, 0)],
        dimension_numbers=('NWC', 'WIO', 'NWC'),
        feature_group_count=C)
    return y + b


def setup_inputs(seed: int = 0) -> dict:
    key = jax.random.key(seed)
    ks = jax.random.split(key, 32)
    f32 = jnp.float32

    def nrm(k, shape, scale):
        return jax.random.normal(k, shape, f32) * scale

    def gain(k, shape):
        return 1.0 + 0.02 * jax.random.normal(k, shape, f32)

    L, D = DEPTH, D_MODEL
    return {
        "x": nrm(ks[0], (BATCH, SEQ, D), 1.0),
        "ffn1_norm_g": gain(ks[1], (L, D)),
        "ffn1_w_gate": nrm(ks[2], (L, D, D_FF), D ** -0.5),
        "ffn1_w_up": nrm(ks[3], (L, D, D_FF), D ** -0.5),
        "ffn1_w_down": nrm(ks[4], (L, D_FF, D), D_FF ** -0.5),
        "mix_norm_g": gain(ks[5], (L, D)),
        "w_in": nrm(ks[6], (L, D, N_IN), D ** -0.5),
        "rel_bias_table": nrm(ks[7], (NUM_BUCKETS, N_HEADS), 0.5),
        "lambda_q1": nrm(ks[8], (L, HEAD_DIM), 0.1),
        "lambda_k1": nrm(ks[9], (L, HEAD_DIM), 0.1),
        "lambda_q2": nrm(ks[10], (L, HEAD_DIM), 0.1),
        "lambda_k2": nrm(ks[11], (L, HEAD_DIM), 0.1),
        "attn_head_norm_g": gain(ks[12], (L, 2 * HEAD_DIM)),
        "w_attn_branch": nrm(ks[13], (L, ATTN_W, D), ATTN_W ** -0.5),
        "conv_dw_w": nrm(ks[14], (L, CONV_WIDTH, CONV_CH), CONV_WIDTH ** -0.5),
        "conv_dw_b": nrm(ks[15], (L, CONV_CH), 0.02),
        "conv_ln_g": gain(ks[16], (L, CONV_CH)),
        "conv_ln_b": nrm(ks[17], (L, CONV_CH), 0.02),
        "w_conv_branch": nrm(ks[18], (L, CONV_CH, D), CONV_CH ** -0.5),
        "w_out": nrm(ks[19], (L, D, D), D ** -0.5),
        "ffn2_norm_g": gain(ks[20], (L, D)),
        "ffn2_w_gate": nrm(ks[21], (L, D, D_FF), D ** -0.5),
        "ffn2_w_up": nrm(ks[22], (L, D, D_FF), D ** -0.5),
        "ffn2_w_down": nrm(ks[23], (L, D_FF, D), D_FF ** -0.5),
        "final_norm_g": gain(ks[24], (D,)),
    }


def reference(x, ffn1_norm_g, ffn1_w_gate, ffn1_w_up, ffn1_w_down, mix_norm_g, w_in,
              rel_bias_table, lambda_q1, lambda_k1, lambda_q2, lambda_k2, attn_head_norm_g,
              w_attn_branch, conv_dw_w, conv_dw_b, conv_ln_g, conv_ln_b, w_conv_branch,
              w_out, ffn2_norm_g, ffn2_w_gate, ffn2_w_up, ffn2_w_down, final_norm_g):
    B, S, D = x.shape
    splits = np.cumsum([ATTN_W, ATTN_W, ATTN_W, CONV_CH, CONV_CH, D_MODEL]).tolist()
    for l in range(DEPTH):
        lam_init = 0.8 - 0.6 * math.exp(-0.3 * l)

        x = x + 0.5 * swiglu(rms_norm(x, ffn1_norm_g[l]), ffn1_w_gate[l], ffn1_w_up[l], ffn1_w_down[l])

        h = rms_norm(x, mix_norm_g[l])
        z = h @ w_in[l]
        q, k, v, glu_a, glu_b, gate_a, gate_c = jnp.split(z, splits, axis=-1)

        q = q.reshape(B, S, N_HEADS, 2, HEAD_DIM).transpose(3, 0, 2, 1, 4)
        k = k.reshape(B, S, N_HEADS, 2, HEAD_DIM).transpose(3, 0, 2, 1, 4)
        v = v.reshape(B, S, N_HEADS, 2 * HEAD_DIM).transpose(0, 2, 1, 3)
        lq1 = lambda_q1[l].astype(jnp.float32)
        lk1 = lambda_k1[l].astype(jnp.float32)
        lq2 = lambda_q2[l].astype(jnp.float32)
        lk2 = lambda_k2[l].astype(jnp.float32)
        lam = jnp.exp(jnp.sum(lq1 * lk1)) - jnp.exp(jnp.sum(lq2 * lk2)) + lam_init
        o = diff_attention(q[0], q[1], k[0], k[1], v, lam, rel_bias_table)
        o = rms_norm(o, attn_head_norm_g[l], eps=1e-5) * (1.0 - lam_init)
        attn_branch = o.reshape(B, S, ATTN_W) @ w_attn_branch[l]

        u = glu_a * jax.nn.sigmoid(glu_b)
        u = causal_depthwise_conv(u, conv_dw_w[l], conv_dw_b[l])
        u = jax.nn.silu(layer_norm(u, conv_ln_g[l], conv_ln_b[l]))
        conv_branch = u @ w_conv_branch[l]

        merged = jax.nn.sigmoid(gate_a) * attn_branch + jax.nn.sigmoid(gate_c) * conv_branch
        x = x + merged @ w_out[l]

        x = x + 0.5 * swiglu(rms_norm(x, ffn2_norm_g[l]), ffn2_w_gate[l], ffn2_w_up[l], ffn2_w_down[l])

    return rms_norm(x, final_norm_g)
```

```python
import math
import numpy as np
import concourse.bass as bass
import concourse.mybir as mybir
from concourse.bass_utils import run_bass_kernel_spmd

F32 = mybir.dt.float32
BF16 = mybir.dt.bfloat16
AF = mybir.ActivationFunctionType
ALU = mybir.AluOpType
AX = mybir.AxisListType

D = 2048
DC = 16
DFF = 5632
FC = 44
NH = 8
TT = 512
MASKV = -30000.0
LAM_INIT = 0.8 - 0.6 * math.exp(0.0)
NQ = 6

C_G1, C_GM, C_G2, C_GF = 0, 16, 32, 48
C_CB, C_LG, C_LB, C_HG = 64, 72, 80, 88
C_LAM = 96
C_FB = 352
C_CW = 368
C_ID = 616
C_N = 744


class Buf:
    __slots__ = ("w", "r", "name")

    def __init__(self, name=""):
        self.w = {}
        self.r = {}
        self.name = name


class Prog:
    ENG = ("pe", "act", "dve", "pool", "sp")

    def __init__(self, nc, esems, qsems):
        self.nc = nc
        self.streams = {e: [] for e in self.ENG}
        self.sem = esems
        self.cnt = {e: 0 for e in self.ENG}
        self.waited = {e: {} for e in self.ENG}
        self.qsems = qsems
        self.qidx = {q: 0 for q in qsems}
        self.qcum = {q: [0] * len(qsems[q]) for q in qsems}
        self.nwaits = 0

    def _wait(self, eng, toks):
        for key, (h, v) in toks.items():
            if eng == "pe" and key == "pe":
                continue
            if self.waited[eng].get(key, 0) >= v:
                continue
            self.waited[eng][key] = v
            self.nwaits += 1
            self.streams[eng].append(lambda e, h=h, v=v: e.wait_ge(h, v))

    def deps(self, eng, reads, writes, pwrites=()):
        for b in reads:
            self._wait(eng, b.w)
        for b in writes:
            self._wait(eng, b.w)
            self._wait(eng, b.r)
        for b in pwrites:
            self._wait(eng, b.r)

    def _post(self, key, tok, reads, writes, pwrites):
        for b in reads:
            b.r[key] = tok
        for b in writes:
            b.w = {key: tok}
            b.r = {}
        for b in pwrites:
            b.w[key] = tok

    def op(self, eng, fn, reads=(), writes=(), pwrites=()):
        self.deps(eng, reads, writes, pwrites)
        self.cnt[eng] += 1
        h = self.sem[eng]
        tok = (h, self.cnt[eng])
        self.streams[eng].append(lambda e, fn=fn, h=h: fn(e).then_inc(h, 1))
        self._post(eng, tok, reads, writes, pwrites)

    def pe_group(self, mms, reads, writes, pwrites=()):
        eng = "pe"
        self.deps(eng, reads, writes, pwrites)
        self.cnt[eng] += 1
        h = self.sem[eng]
        tok = (h, self.cnt[eng])
        n = len(mms)
        for i, (o, l, r, st, sp) in enumerate(mms):
            if i == n - 1:
                self.streams[eng].append(
                    lambda e, o=o, l=l, r=r, st=st, sp=sp, h=h: e.matmul(o, lhsT=l, rhs=r, start=st, stop=sp).then_inc(h, 1))
            else:
                self.streams[eng].append(
                    lambda e, o=o, l=l, r=r, st=st, sp=sp: e.matmul(o, lhsT=l, rhs=r, start=st, stop=sp))
        self._post(eng, tok, reads, writes, pwrites)

    def dma(self, q, out_ap, in_ap, reads=(), writes=(), pwrites=()):
        i = self.qidx[q]
        self.qidx[q] = (i + 1) % len(self.qsems[q])
        h = self.qsems[q][i]
        key = (q, i)
        prev = self.qcum[q][i]
        if prev:
            self._wait(q, {key: (h, prev)})
        self.deps(q, reads, writes, pwrites)
        self.qcum[q][i] += 16
        tok = (h, self.qcum[q][i])
        self.streams[q].append(lambda e, o=out_ap, a=in_ap, h=h: e.dma_start(out=o, in_=a).then_inc(h, 16))
        self._post(key, tok, reads, writes, pwrites)

    def all_tokens(self):
        t = {}
        for e in self.ENG:
            if self.cnt[e]:
                t[e] = (self.sem[e], self.cnt[e])
        for q in self.qsems:
            for i, h in enumerate(self.qsems[q]):
                if self.qcum[q][i]:
                    t[(q, i)] = (h, self.qcum[q][i])
        return t

    def barrier(self, engs=None):
        t = self.all_tokens()
        for e in (engs or self.ENG):
            self._wait(e, t)


def build(NT):
    NP = NT
    HALF = NT * TT
    NKEY = 2 * HALF
    NPB = NP * 4
    nc = bass.Bass("TRN2", target_bir_lowering=False)

    def din(name, shape, dt=F32):
        return nc.dram_tensor(name, list(shape), dt, kind="ExternalInput").ap()

    xT_own = din("xT_own", [128, DC, HALF])
    xT_pre = din("xT_pre", [128, DC, HALF])
    w_g1 = din("w_g1", [D, DFF]); w_u1 = din("w_u1", [D, DFF]); w_d1 = din("w_d1", [DFF, D])
    w_g2 = din("w_g2", [D, DFF]); w_u2 = din("w_u2", [D, DFF]); w_d2 = din("w_d2", [DFF, D])
    w_in = din("w_in", [D, 9216])
    w_ab = din("w_ab", [1024, D]); w_cb = din("w_cb", [1024, D]); w_o = din("w_o", [D, D])
    cpack_d = din("cpack", [128, C_N])
    btile_d = din("btile", [NH, 128, 384])
    outT = nc.dram_tensor("outT", [128, DC, HALF], F32, kind="ExternalOutput").ap()
    KT_scr = nc.dram_tensor("KT_scr", [NH, 128, NKEY], BF16).ap()
    V_scr = nc.dram_tensor("V_scr", [NKEY, 1024], BF16).ap()

    import contextlib
    es = contextlib.ExitStack()
    with es:
        def sb(name, shape, dt):
            return es.enter_context(nc.sbuf_tensor(name, list(shape), dt))

        xT = sb("xT", [128, DC, TT], F32)
        hT = sb("hT", [128, DC, TT], BF16)
        uT = sb("uT", [128, 8, 544], BF16)
        wbuf = sb("wbuf", [128, 2, 16 * 512], BF16)
        cp = sb("cp", [128, C_N], F32)
        bc = sb("bc", [128, 3, TT], F32)
        stg = sb("stg", [128, 2, TT], BF16)
        sqs = sb("sqs", [128, 2, TT], F32)
        onesD = sb("onesD", [128, 128], F32)
        onesC = sb("onesC", [128, 128], F32)
        onesH = sb("onesH", [128, 128], F32)
        onesB = sb("onesB", [128, 128], BF16)
        sm = sb("sm", [128, 16], F32)
        lt = sb("lt", [128, 64], F32)
        AR_W = 93 * 256
        arena = sb("arena", [128, AR_W], F32)
        ps = [es.enter_context(nc.psum_tensor(f"ps{i}", [128, TT], F32)) for i in range(8)]
        esems = {e: es.enter_context(nc.semaphore(f"s_{e}")) for e in Prog.ENG}
        qsems = {q: [es.enter_context(nc.semaphore(f"q_{q}{i}")) for i in range(NQ)] for q in ("sp", "pool")}
        P = Prog(nc, esems, qsems)

        def av(kib0, kib1, dt):
            a = arena[:, kib0 * 256:kib1 * 256]
            return a.bitcast(dt) if dt != F32 else a

        actT = av(0, 44, BF16)
        sgt = av(48, 52, BF16)
        ostg = av(56, 88, F32)
        qT = av(0, 8, BF16)
        sgA = av(8, 24, BF16)
        sgC = av(24, 40, BF16)
        KTb = av(40, 56, BF16)
        yT = av(40, 56, F32)
        Vb = av(56, 64, BF16)
        Pb = av(64, 68, BF16)
        BTb = av(68, 71, F32)
        fscr = av(71, 85, F32)
        Db = av(85, 93, BF16)

        def ch(ap2d, c, w=TT):
            return ap2d[:, c * w:(c + 1) * w]

        b_xT = [Buf(f"xT{c}") for c in range(DC)]
        b_hT = [Buf(f"hT{c}") for c in range(DC)]
        b_act = [Buf(f"act{c}") for c in range(FC)]
        b_sgt = [Buf() for _ in range(4)]
        b_uT = Buf("uT")
        b_w = [Buf("w0"), Buf("w1")]
        b_cp = Buf("cp")
        b_bc = [Buf(), Buf(), Buf()]
        b_stg = [Buf(), Buf()]
        b_sqs = [Buf(), Buf()]
        b_const = Buf("const")
        b_sm = Buf("sm")
        b_lt = Buf("lt")
        b_ps = [Buf(f"ps{i}") for i in range(8)]
        b_KT = [Buf(), Buf()]
        b_V = Buf("V")
        b_P = [Buf() for _ in range(4)]
        b_BT = [Buf(), Buf()]
        b_f = [Buf() for _ in range(7)]
        b_D = Buf("D")
        b_y = [Buf() for _ in range(8)]
        b_ostg = [Buf() for _ in range(DC)]
        b_KTs = Buf("KT_scr")
        b_Vs = Buf("V_scr")
        b_out = Buf("out")

        def cpc(col, n=1):
            return cp[:, col:col + n]

        P.dma("sp", cp[:], cpack_d, writes=[b_cp])
        P.op("pool", lambda e: e.memset(onesD[:], 1.0 / D), writes=[b_const])
        P.op("pool", lambda e: e.memset(onesC[:], 1.0 / 1024.0), writes=[b_const])
        P.op("pool", lambda e: e.memset(onesH[:], 1.0 / 128.0), writes=[b_const])
        P.op("pool", lambda e: e.memset(onesB[:], 1.0), writes=[b_const])
        P.op("dve", lambda e: e.tensor_tensor(out=lt[:], in0=cpc(C_LAM, 64), in1=cpc(C_LAM + 64, 64), op=ALU.mult),
             reads=[b_cp], writes=[b_lt])
        P.op("dve", lambda e: e.reduce_sum(out=sm[:, 2:3], in_=lt[:], axis=AX.X), reads=[b_lt], writes=[b_sm])
        P.op("dve", lambda e: e.tensor_tensor(out=lt[:], in0=cpc(C_LAM + 128, 64), in1=cpc(C_LAM + 192, 64), op=ALU.mult),
             reads=[b_cp, b_sm], writes=[b_lt])
        P.op("dve", lambda e: e.reduce_sum(out=sm[:, 3:4], in_=lt[:], axis=AX.X), reads=[b_lt], writes=[b_sm])
        P.op("act", lambda e: e.activation(out=sm[:, 4:6], in_=sm[:, 2:4], func=AF.Exp), reads=[b_sm], writes=[b_sm])
        P.op("dve", lambda e: e.tensor_tensor(out=sm[:, 6:7], in0=sm[:, 5:6], in1=sm[:, 4:5], op=ALU.subtract),
             reads=[b_sm], writes=[b_sm])
        P.op("dve", lambda e: e.tensor_scalar_add(out=sm[:, 0:1], in0=sm[:, 6:7], scalar1=-LAM_INIT),
             reads=[b_sm], writes=[b_sm])
        P.op("dve", lambda e: e.tensor_scalar_mul(out=sm[:, 1:2], in0=cpc(C_HG), scalar1=1.0 - LAM_INIT),
             reads=[b_sm, b_cp], writes=[b_sm])

        P.op("pool", lambda e: e.memset(sm[:, 8:9], 1e-6), writes=[b_sm])
        P.op("pool", lambda e: e.memset(sm[:, 9:10], 1e-5), writes=[b_sm])

        def rstd_from(src_ap, b_src, epscol, out_ap, b_o):
            P.op("act", lambda e: e.activation(out=out_ap, in_=src_ap, func=AF.Sqrt, bias=sm[:, epscol:epscol + 1]),
                 reads=[b_src, b_sm], writes=[b_o])
            P.op("dve", lambda e: e.reciprocal(out=out_ap, in_=out_ap), reads=[b_o], writes=[b_o])

        wplan = []
        wstate = {"issued": 0, "used": 0}

        def w_issue():
            i = wstate["issued"]
            if i >= len(wplan):
                return
            W, kc0, nkc, col0, ncols = wplan[i]
            s = i % 2
            dst = wbuf[:, s, 0:nkc * ncols].rearrange("p (k m) -> p k m", k=nkc)
            src = W[kc0 * 128:(kc0 + nkc) * 128, col0:col0 + ncols].rearrange("(k p) m -> p k m", p=128)
            P.dma("pool", dst, src, writes=[b_w[s]])
            wstate["issued"] += 1

        def w_get(spec):
            i = wstate["used"]
            assert wplan[i] == spec, (i, wplan[i][1:], spec[1:])
            while wstate["issued"] < min(i + 2, len(wplan)):
                w_issue()
            wstate["used"] += 1
            s = i % 2
            return wbuf[:, s, :], b_w[s]

        def plan_ffn(wg, wu, wd):
            for i in range(DFF // 512):
                wplan.append((wg, 0, 16, i * 512, 512))
                wplan.append((wu, 0, 16, i * 512, 512))
            for j in range(4):
                for (k0, nk) in ((0, 16), (16, 16), (32, 12)):
                    wplan.append((wd, k0, nk, j * 512, 512))

        def plan_lin(W, K, cols):
            for c0 in cols:
                wplan.append((W, 0, K // 128, c0, 512))

        KV_COLS = [1024, 1536, 2048, 2560]
        GLU_COLS = [3072, 4096, 3584, 4608]
        for t in range(NP):
            plan_ffn(w_g1, w_u1, w_d1)
            plan_lin(w_in, D, KV_COLS + (GLU_COLS if t == NP - 1 else []))
        for t in range(NT):
            plan_ffn(w_g1, w_u1, w_d1)
            plan_lin(w_in, D, [0, 512] + KV_COLS + GLU_COLS + [5120 + 512 * i for i in range(8)])
            plan_lin(w_ab, 1024, [0, 512, 1024, 1536])
            plan_lin(w_cb, 1024, [0, 512, 1024, 1536])
            plan_lin(w_o, D, [0, 512, 1024, 1536])
            plan_ffn(w_g2, w_u2, w_d2)

        def lin_group(W, K, col0, in_aps, in_bufs, banks, evac, tokmajor=False, kgroups=None):
            kgroups = kgroups or [(0, K // 128)]
            ng = len(kgroups)
            for gi, (k0, nk) in enumerate(kgroups):
                wap, wb = w_get((W, k0, nk, col0, 512))
                for m in range(4):
                    mms = []
                    for kk in range(nk):
                        st = (gi == 0 and kk == 0)
                        sp = (gi == ng - 1 and kk == nk - 1)
                        if tokmajor:
                            l = in_aps[k0 + kk][:, m * 128:(m + 1) * 128]
                            r = wap[:, kk * 512:(kk + 1) * 512]
                        else:
                            l = wap[:, kk * 512 + m * 128: kk * 512 + (m + 1) * 128]
                            r = in_aps[k0 + kk]
                        mms.append((ps[banks[m]][:], l, r, st, sp))
                    P.pe_group(mms, reads=[wb] + [in_bufs[k0 + kk] for kk in range(nk)],
                               writes=[b_ps[banks[m]]] if gi == 0 else [],
                               pwrites=[] if gi == 0 else [b_ps[banks[m]]])
            for m in range(4):
                evac(m, banks[m])

        def rmsnorm_stats(gdummy=None):
            for c in range(DC):
                s = c % 2
                P.op("act", lambda e, c=c, s=s: e.activation(out=sqs[:, s, :], in_=xT[:, c, :], func=AF.Square),
                     reads=[b_xT[c]], writes=[b_sqs[s]])
                P.pe_group([(ps[0][:], onesD[:], sqs[:, s, :], c == 0, c == DC - 1)],
                           reads=[b_sqs[s], b_const], writes=[b_ps[0]] if c == 0 else [],
                           pwrites=[] if c == 0 else [b_ps[0]])
            rstd_from(ps[0][:], b_ps[0], 8, bc[:, 0, :], b_bc[0])

        def norm_to_hT(gcol):
            rmsnorm_stats()
            for c in range(DC):
                P.op("dve", lambda e, c=c: e.scalar_tensor_tensor(out=hT[:, c, :], in0=xT[:, c, :], scalar=cpc(gcol + c),
                                                                 in1=bc[:, 0, :], op0=ALU.mult, op1=ALU.mult),
                     reads=[b_xT[c], b_bc[0], b_cp], writes=[b_hT[c]])

        hT_aps = [hT[:, c, :] for c in range(DC)]
        act_aps = [ch(actT, f) for f in range(FC)]

        def ffn(wg, wu, wd, gcol):
            norm_to_hT(gcol)
            for i in range(DFF // 512):
                def ev_gate(m, bank):
                    P.op("act", lambda e, m=m, bank=bank: e.activation(out=ch(sgt, m), in_=ps[bank][:], func=AF.Silu),
                         reads=[b_ps[bank]], writes=[b_sgt[m]])

                def ev_up(m, bank, i=i):
                    f = 4 * i + m
                    P.op("dve", lambda e, m=m, bank=bank, f=f: e.tensor_tensor(out=act_aps[f], in0=ps[bank][:],
                                                                             in1=ch(sgt, m), op=ALU.mult),
                         reads=[b_ps[bank], b_sgt[m]], writes=[b_act[f]])
                lin_group(wg, D, i * 512, hT_aps, b_hT, [0, 1, 2, 3], ev_gate)
                lin_group(wu, D, i * 512, hT_aps, b_hT, [4, 5, 6, 7], ev_up)
            for j in range(4):
                def ev_dn(m, bank, j=j):
                    c = 4 * j + m
                    P.op("dve", lambda e, c=c, bank=bank: e.scalar_tensor_tensor(out=xT[:, c, :], in0=ps[bank][:], scalar=0.5,
                                                                               in1=xT[:, c, :], op0=ALU.mult, op1=ALU.add),
                         reads=[b_ps[bank], b_xT[c]], writes=[b_xT[c]])
                banks = [0, 1, 2, 3] if j % 2 == 0 else [4, 5, 6, 7]
                lin_group(wd, DFF, j * 512, act_aps, b_act, banks, ev_dn, kgroups=[(0, 16), (16, 16), (32, 12)])

        def load_x(src, t):
            P.dma("sp", xT[:], src[:, :, t * TT:(t + 1) * TT], writes=b_xT)

        bank_rr = {"i": 0}

        def next_banks():
            bank_rr["i"] ^= 1
            return [0, 1, 2, 3] if bank_rr["i"] else [4, 5, 6, 7]

        def proj_kv(pos):
            for g in range(2):
                def ev_k(m, bank, g=g):
                    h = 4 * g + m
                    s = h % 2
                    P.op("act", lambda e, s=s, bank=bank: e.activation(out=stg[:, s, :], in_=ps[bank][:], func=AF.Copy),
                         reads=[b_ps[bank]], writes=[b_stg[s]])
                    P.dma("sp", KT_scr[h, :, pos:pos + TT], stg[:, s, :], reads=[b_stg[s]], pwrites=[b_KTs])
                lin_group(w_in, D, 1024 + g * 512, hT_aps, b_hT, next_banks(), ev_k)
            for g in range(2):
                def ev_v(m, bank, g=g):
                    s = m % 2
                    P.op("dve", lambda e, s=s, bank=bank: e.tensor_copy(out=stg[:, s, :], in_=ps[bank][:]),
                         reads=[b_ps[bank]], writes=[b_stg[s]])
                    P.dma("sp", V_scr[pos + m * 128:pos + (m + 1) * 128, g * 512:(g + 1) * 512], stg[:, s, :],
                          reads=[b_stg[s]], pwrites=[b_Vs])
                lin_group(w_in, D, 2048 + g * 512, hT_aps, b_hT, next_banks(), ev_v, tokmajor=True)

        def proj_glu():
            P.op("dve", lambda e: e.tensor_copy(out=uT[:, :, 0:30], in_=uT[:, :, 512:542]), reads=[b_uT], writes=[b_uT])
            for g in range(2):
                ba = next_banks()
                bb = next_banks()

                def ev_a(m, bank):
                    pass

                def ev_b(m, bank, g=g, ba=ba):
                    c = 4 * g + m
                    s = m % 2
                    P.op("act", lambda e, bank=bank, s=s: e.activation(out=sqs[:, s, :], in_=ps[bank][:], func=AF.Sigmoid),
                         reads=[b_ps[bank]], writes=[b_sqs[s]])
                    P.op("dve", lambda e, c=c, s=s, m=m: e.tensor_tensor(out=uT[:, c, 30:542], in0=ps[ba[m]][:],
                                                                        in1=sqs[:, s, :], op=ALU.mult),
                         reads=[b_ps[ba[m]], b_sqs[s]], writes=[b_uT])
                lin_group(w_in, D, 3072 + g * 512, hT_aps, b_hT, ba, ev_a)
                lin_group(w_in, D, 4096 + g * 512, hT_aps, b_hT, bb, ev_b)

        import os
        STOP = int(os.environ.get("DEV_STOP", "0"))

        class _Stop(Exception):
            pass

        def checkpoint(k):
            if STOP != k:
                return
            P.barrier()
            for c in range(DC):
                P.op("dve", lambda e, c=c: e.tensor_copy(out=ch(ostg, c), in_=xT[:, c, :]),
                     reads=[b_xT[c]], writes=[b_ostg[c]])
            P.dma("sp", outT[:, :, 0:TT], ostg.rearrange("p (c t) -> p c t", c=DC), reads=b_ostg, pwrites=[b_out])
            raise _Stop()

        def emit_all():
            P.op("pool", lambda e: e.memset(uT[:], 0.0), writes=[b_uT])
            for t in range(NP):
                load_x(xT_pre, t)
                if t == 0:
                    norm_to_hT(C_G1)
                    checkpoint(1)
                ffn(w_g1, w_u1, w_d1, C_G1)
                checkpoint(2)
                norm_to_hT(C_GM)
                proj_kv(t * TT)
                if t == NP - 1:
                    proj_glu()
                checkpoint(3)
                P.barrier()

            q_aps = [ch(qT, h) for h in range(NH)]
            b_q = b_act[0:8]
            sgA_aps = [ch(sgA, c) for c in range(DC)]
            b_sgA = b_act[8:24]
            sgC_aps = [ch(sgC, c) for c in range(DC)]
            b_sgC = b_act[24:40]
            on_aps = [hT[:, h, :] for h in range(8)]
            b_on = b_hT[0:8]
            c_aps = [hT[:, 8 + c, :] for c in range(8)]
            b_c = b_hT[8:16]
            f_aps = [ch(fscr, i) for i in range(7)]
            y_aps = [ch(yT, i) for i in range(8)]

            for t in range(NT):
                pos = HALF + t * TT
                G0 = pos // 128
                load_x(xT_own, t)
                ffn(w_g1, w_u1, w_d1, C_G1)
                P.barrier()
                norm_to_hT(C_GM)
                for g in range(2):
                    def ev_q(m, bank, g=g):
                        h = 4 * g + m
                        P.op("act", lambda e, h=h, bank=bank: e.activation(out=q_aps[h], in_=ps[bank][:], func=AF.Copy),
                             reads=[b_ps[bank]], writes=[b_q[h]])
                    lin_group(w_in, D, g * 512, hT_aps, b_hT, next_banks(), ev_q)
                proj_kv(pos)
                proj_glu()
                checkpoint(4)
                for i in range(8):
                    def ev_s(m, bank, i=i):
                        c = (4 * i + m) % 16
                        dst, bb_ = (sgA_aps[c], b_sgA[c]) if i < 4 else (sgC_aps[c], b_sgC[c])
                        P.op("act", lambda e, dst=dst, bank=bank: e.activation(out=dst, in_=ps[bank][:], func=AF.Sigmoid),
                             reads=[b_ps[bank]], writes=[bb_])
                    lin_group(w_in, D, 5120 + 512 * i, hT_aps, b_hT, next_banks(), ev_s)
                P.barrier()
                nkb = G0 + 4
                for h in range(NH):
                    kb_ = h % 2
                    KTh = KTb[:, kb_ * 4096: kb_ * 4096 + nkb * 128]
                    P.dma("sp", KTh, KT_scr[h, :, 0:nkb * 128], reads=[b_KTs], writes=[b_KT[kb_]])
                    P.dma("sp", Vb[:, 0:nkb * 128].rearrange("p (b d) -> p b d", d=128),
                          V_scr[0:nkb * 128, h * 128:(h + 1) * 128].rearrange("(b p) d -> p b d", p=128),
                          reads=[b_Vs], writes=[b_V])
                    bt_ = h % 2
                    P.dma("sp", BTb[:, bt_ * 384:(bt_ + 1) * 384], btile_d[h], writes=[b_BT[bt_]])

                    def s_stage(kb, sset):
                        qi0 = max(0, kb - G0)
                        qlo = qi0 * 128
                        s1, s2 = (0, 1) if sset == 0 else (2, 3)
                        ks = slice(kb * 128, (kb + 1) * 128)
                        P.pe_group([(ps[s1][:, qlo:TT], KTb[0:64, kb_ * 4096 + kb * 128: kb_ * 4096 + (kb + 1) * 128],
                                     qT[0:64, h * TT + qlo:(h + 1) * TT], True, True)],
                                   reads=[b_KT[kb_], b_q[h]], writes=[b_ps[s1]])
                        P.pe_group([(ps[s2][:, qlo:TT], KTb[64:128, kb_ * 4096 + kb * 128: kb_ * 4096 + (kb + 1) * 128],
                                     qT[64:128, h * TT + qlo:(h + 1) * TT], True, True)],
                                   reads=[b_KT[kb_], b_q[h]], writes=[b_ps[s2]])
                        fbcol = C_FB + (8 + h if kb < NPB else h)
                        for which, bank in ((0, s1), (1, s2)):
                            pi = sset * 2 + which
                            pap = ch(Pb, pi)
                            far_lo = None
                            first = True
                            for qi in range(qi0, 4):
                                dblk = G0 + qi - kb
                                if dblk >= 2:
                                    far_lo = qi * 128
                                    break
                                typ = 0 if dblk == 0 else (2 if (kb == NPB - 1) else 1)
                                tmp_i = 5 if which == 0 else 6
                                tsl = f_aps[tmp_i][:, qi * 128:(qi + 1) * 128]
                                btap = BTb[:, bt_ * 384 + typ * 128: bt_ * 384 + (typ + 1) * 128]
                                P.op("dve", lambda e, tsl=tsl, bank=bank, qi=qi, btap=btap: e.scalar_tensor_tensor(
                                    out=tsl, in0=ps[bank][:, qi * 128:(qi + 1) * 128], scalar=0.125,
                                    in1=btap, op0=ALU.mult, op1=ALU.add),
                                    reads=[b_ps[bank], b_BT[bt_]], writes=[b_f[tmp_i]])
                                P.op("act", lambda e, tsl=tsl, pap=pap, qi=qi: e.activation(
                                    out=pap[:, qi * 128:(qi + 1) * 128], in_=tsl, func=AF.Exp),
                                    reads=[b_f[tmp_i]], writes=[b_P[pi]] if first else [], pwrites=[] if first else [b_P[pi]])
                                first = False
                            if far_lo is not None:
                                P.op("act", lambda e, pap=pap, bank=bank, far_lo=far_lo, fbcol=fbcol: e.activation(
                                    out=pap[:, far_lo:TT], in_=ps[bank][:, far_lo:TT], func=AF.Exp,
                                    bias=cpc(fbcol), scale=0.125),
                                    reads=[b_ps[bank], b_cp], writes=[b_P[pi]] if first else [], pwrites=[] if first else [b_P[pi]])
                        return qlo

                    def av_stage(kb, sset, qlo, first, last):
                        for which in (0, 1):
                            pi = sset * 2 + which
                            pap = ch(Pb, pi)[:, qlo:TT]
                            ob = 4 + which
                            lb = 6 + which
                            P.pe_group([(ps[ob][:, qlo:TT], Vb[:, kb * 128:(kb + 1) * 128], pap, first, last)],
                                       reads=[b_V, b_P[pi]], writes=[b_ps[ob]] if first else [],
                                       pwrites=[] if first else [b_ps[ob]])
                            P.pe_group([(ps[lb][:, qlo:TT], onesB[:], pap, first, last)],
                                       reads=[b_P[pi], b_const], writes=[b_ps[lb]] if first else [],
                                       pwrites=[] if first else [b_ps[lb]])

                    prev = None
                    for kb in range(nkb):
                        sset = kb % 2
                        qlo = s_stage(kb, sset)
                        if prev is not None:
                            av_stage(prev[0], prev[1], prev[2], prev[0] == 0, False)
                        prev = (kb, sset, qlo)
                    av_stage(prev[0], prev[1], prev[2], prev[0] == 0, True)
                    P.op("dve", lambda e: e.reciprocal(out=f_aps[0], in_=ps[6][:]), reads=[b_ps[6]], writes=[b_f[0]])
                    P.op("dve", lambda e: e.reciprocal(out=f_aps[1], in_=ps[7][:]), reads=[b_ps[7]], writes=[b_f[1]])
                    P.op("dve", lambda e: e.tensor_tensor(out=f_aps[2], in0=ps[4][:], in1=f_aps[0], op=ALU.mult),
                         reads=[b_ps[4], b_f[0]], writes=[b_f[2]])
                    P.op("dve", lambda e: e.tensor_tensor(out=f_aps[3], in0=ps[5][:], in1=f_aps[1], op=ALU.mult),
                         reads=[b_ps[5], b_f[1]], writes=[b_f[3]])
                    P.op("dve", lambda e: e.scalar_tensor_tensor(out=f_aps[4], in0=f_aps[3], scalar=sm[:, 0:1], in1=f_aps[2],
                                                                 op0=ALU.mult, op1=ALU.add),
                         reads=[b_f[2], b_f[3], b_sm], writes=[b_f[4]])
                    P.op("act", lambda e: e.activation(out=f_aps[0], in_=f_aps[4], func=AF.Square),
                         reads=[b_f[4]], writes=[b_f[0]])
                    P.pe_group([(ps[0][:], onesH[:], f_aps[0], True, True)], reads=[b_f[0], b_const], writes=[b_ps[0]])
                    rstd_from(ps[0][:], b_ps[0], 9, bc[:, 1, :], b_bc[1])
                    P.op("dve", lambda e, h=h: e.scalar_tensor_tensor(out=on_aps[h], in0=f_aps[4], scalar=sm[:, 1:2],
                                                                     in1=bc[:, 1, :], op0=ALU.mult, op1=ALU.mult),
                         reads=[b_f[4], b_bc[1], b_sm], writes=[b_on[h]])
                checkpoint(5)
                P.barrier()
                for j in range(4):
                    def ev_ab(m, bank, j=j):
                        c = 4 * j + m
                        P.op("dve", lambda e, c=c, bank=bank: e.tensor_tensor(out=sgA_aps[c], in0=ps[bank][:], in1=sgA_aps[c],
                                                                            op=ALU.mult),
                             reads=[b_ps[bank], b_sgA[c]], writes=[b_sgA[c]])
                    lin_group(w_ab, 1024, j * 512, on_aps, b_on, next_banks(), ev_ab)
                for c in range(8):
                    for j in range(31):
                        P.op("dve", lambda e, c=c, j=j: e.tensor_scalar_mul(out=Db[:, j * 128:(j + 1) * 128],
                                                                           in0=cpc(C_ID, 128), scalar1=cpc(C_CW + c * 31 + j)),
                             reads=[b_cp], writes=[b_D] if j == 0 else [], pwrites=[] if j == 0 else [b_D])
                    bank = 0 if c % 2 == 0 else 1
                    mms = [(ps[bank][:], Db[:, j * 128:(j + 1) * 128], uT[:, c, j:j + TT], j == 0, j == 30) for j in range(31)]
                    P.pe_group(mms, reads=[b_D, b_uT], writes=[b_ps[bank]])
                    P.op("act", lambda e, c=c, bank=bank: e.activation(out=y_aps[c], in_=ps[bank][:], func=AF.Identity,
                                                                      bias=cpc(C_CB + c)),
                         reads=[b_ps[bank], b_cp], writes=[b_y[c]])
                    P.pe_group([(ps[2][:], onesC[:], y_aps[c], c == 0, c == 7)], reads=[b_y[c], b_const],
                               writes=[b_ps[2]] if c == 0 else [], pwrites=[] if c == 0 else [b_ps[2]])
                    s = c % 2
                    P.op("act", lambda e, c=c, s=s: e.activation(out=sqs[:, s, :], in_=y_aps[c], func=AF.Square),
                         reads=[b_y[c]], writes=[b_sqs[s]])
                    P.pe_group([(ps[3][:], onesC[:], sqs[:, s, :], c == 0, c == 7)], reads=[b_sqs[s], b_const],
                               writes=[b_ps[3]] if c == 0 else [], pwrites=[] if c == 0 else [b_ps[3]])
                P.op("dve", lambda e: e.tensor_copy(out=bc[:, 2, :], in_=ps[2][:]), reads=[b_ps[2]], writes=[b_bc[2]])
                P.op("dve", lambda e: e.tensor_tensor(out=bc[:, 1, :], in0=bc[:, 2, :], in1=bc[:, 2, :], op=ALU.mult),
                     reads=[b_bc[2]], writes=[b_bc[1]])
                P.op("dve", lambda e: e.tensor_tensor(out=bc[:, 1, :], in0=ps[3][:], in1=bc[:, 1, :], op=ALU.subtract),
                     reads=[b_ps[3], b_bc[1]], writes=[b_bc[1]])
                rstd_from(bc[:, 1, :], b_bc[1], 9, bc[:, 1, :], b_bc[1])
                for c in range(8):
                    P.op("dve", lambda e, c=c: e.tensor_tensor(out=y_aps[c], in0=y_aps[c], in1=bc[:, 2, :], op=ALU.subtract),
                         reads=[b_y[c], b_bc[2]], writes=[b_y[c]])
                    P.op("dve", lambda e, c=c: e.tensor_tensor(out=y_aps[c], in0=y_aps[c], in1=bc[:, 1, :], op=ALU.mult),
                         reads=[b_y[c], b_bc[1]], writes=[b_y[c]])
                    P.op("act", lambda e, c=c: e.activation(out=c_aps[c], in_=y_aps[c], func=AF.Silu,
                                                           bias=cpc(C_LB + c), scale=cpc(C_LG + c)),
                         reads=[b_y[c], b_cp], writes=[b_c[c]])
                for j in range(4):
                    def ev_cb(m, bank, j=j):
                        c = 4 * j + m
                        P.op("dve", lambda e, c=c, bank=bank: e.tensor_tensor(out=sgC_aps[c], in0=ps[bank][:], in1=sgC_aps[c],
                                                                            op=ALU.mult),
                             reads=[b_ps[bank], b_sgC[c]], writes=[b_sgC[c]])
                        P.op("dve", lambda e, c=c: e.tensor_tensor(out=sgA_aps[c], in0=sgA_aps[c], in1=sgC_aps[c], op=ALU.add),
                             reads=[b_sgA[c], b_sgC[c]], writes=[b_sgA[c]])
                    lin_group(w_cb, 1024, j * 512, c_aps, b_c, next_banks(), ev_cb)
                for j in range(4):
                    def ev_o(m, bank, j=j):
                        c = 4 * j + m
                        P.op("dve", lambda e, c=c, bank=bank: e.tensor_tensor(out=xT[:, c, :], in0=ps[bank][:], in1=xT[:, c, :],
                                                                            op=ALU.add),
                             reads=[b_ps[bank], b_xT[c]], writes=[b_xT[c]])
                    lin_group(w_o, D, j * 512, sgA_aps, b_sgA, next_banks(), ev_o)
                P.barrier()
                checkpoint(6)
                ffn(w_g2, w_u2, w_d2, C_G2)
                rmsnorm_stats()
                for c in range(DC):
                    P.op("dve", lambda e, c=c: e.scalar_tensor_tensor(out=ch(ostg, c), in0=xT[:, c, :], scalar=cpc(C_GF + c),
                                                                     in1=bc[:, 0, :], op0=ALU.mult, op1=ALU.mult),
                         reads=[b_xT[c], b_bc[0], b_cp], writes=[b_ostg[c]])
                P.dma("sp", outT[:, :, t * TT:(t + 1) * TT], ostg.rearrange("p (c t) -> p c t", c=DC),
                      reads=b_ostg, pwrites=[b_out])
                P.barrier()


        try:
            emit_all()
        except _Stop:
            wstate["used"] = len(wplan)

        assert wstate["used"] == len(wplan), (wstate, len(wplan))
        P.barrier(["sp"])

        with nc.Block() as block:
            @block.tensor
            def _(e):
                for f in P.streams["pe"]:
                    f(e)

            @block.scalar
            def _(e):
                for f in P.streams["act"]:
                    f(e)

            @block.vector
            def _(e):
                for f in P.streams["dve"]:
                    f(e)

            @block.gpsimd
            def _(e):
                for f in P.streams["pool"]:
                    f(e)

            @block.sync
            def _(e):
                for f in P.streams["sp"]:
                    f(e)
    return nc


def _t5_bucket(n):
    n = np.asarray(n, dtype=np.int64)
    nf = np.maximum(n, 1).astype(np.float32)
    large = 16 + (np.log(nf / np.float32(16)) / np.float32(math.log(8.0)) * np.float32(16)).astype(np.int32)
    large = np.minimum(large, 31)
    return np.where(n < 16, n, large).astype(np.int64)


def _fm(x2d):
    T, F = x2d.shape
    return np.ascontiguousarray(x2d.reshape(T, F // 128, 128).transpose(2, 1, 0))


def _pc(v):
    return np.ascontiguousarray(np.asarray(v, np.float32).reshape(-1, 128).T)


def run(inputs, trace=False):
    x = np.asarray(inputs["x"], np.float32)
    B, S, _ = x.shape
    HALF = S // 2
    NT = HALF // TT
    assert HALF % TT == 0 and B == 4
    f32 = lambda k: np.ascontiguousarray(np.asarray(inputs[k], np.float32))
    table = f32("rel_bias_table")
    ii = np.arange(128)[:, None]
    jj = np.arange(128)[None, :]
    n_diag = jj - ii
    n_near = 128 + jj - ii
    idx_diag = _t5_bucket(np.maximum(n_diag, 0))
    idx_near = _t5_bucket(n_near)
    convw = f32("conv_dw_w")[0]
    base = np.zeros((128, C_N), np.float32)
    base[:, C_G1:C_G1 + 16] = _pc(f32("ffn1_norm_g")[0])
    base[:, C_GM:C_GM + 16] = _pc(f32("mix_norm_g")[0])
    base[:, C_G2:C_G2 + 16] = _pc(f32("ffn2_norm_g")[0])
    base[:, C_GF:C_GF + 16] = _pc(f32("final_norm_g"))
    base[:, C_CB:C_CB + 8] = _pc(f32("conv_dw_b")[0])
    base[:, C_LG:C_LG + 8] = _pc(f32("conv_ln_g")[0])
    base[:, C_LB:C_LB + 8] = _pc(f32("conv_ln_b")[0])
    base[:, C_HG] = f32("attn_head_norm_g")[0]
    for i, k in enumerate(("lambda_q1", "lambda_k1", "lambda_q2", "lambda_k2")):
        base[:, C_LAM + 64 * i:C_LAM + 64 * (i + 1)] = f32(k)[0][None, :]
    cw = convw.reshape(31, 8, 128).transpose(2, 1, 0).reshape(128, 8 * 31)
    base[:, C_CW:C_CW + 248] = cw
    base[:, C_ID:C_ID + 128] = np.eye(128, dtype=np.float32)
    shared = {
        "w_g1": f32("ffn1_w_gate")[0], "w_u1": f32("ffn1_w_up")[0], "w_d1": f32("ffn1_w_down")[0],
        "w_g2": f32("ffn2_w_gate")[0], "w_u2": f32("ffn2_w_up")[0], "w_d2": f32("ffn2_w_down")[0],
        "w_in": f32("w_in")[0], "w_ab": f32("w_attn_branch")[0], "w_cb": f32("w_conv_branch")[0],
        "w_o": f32("w_out")[0],
    }
    in_maps = []
    for c in range(8):
        b, r = c // 2, c % 2
        cpk = base.copy()
        bt = np.empty((NH, 128, 384), np.float32)
        for h in range(NH):
            tb = table[:, h]
            bt[h, :, 0:128] = np.where(n_diag >= 0, tb[idx_diag], np.float32(MASKV))
            near = tb[idx_near]
            bt[h, :, 128:256] = near
            bt[h, :, 256:384] = near if r == 1 else np.float32(MASKV)
            cpk[:, C_FB + h] = tb[31]
            cpk[:, C_FB + 8 + h] = tb[31] if r == 1 else np.float32(MASKV)
        own = x[b, r * HALF:(r + 1) * HALF]
        pre = x[b, 0:HALF] if r == 1 else np.zeros((HALF, D), np.float32)
        m = dict(shared)
        m["xT_own"] = _fm(own)
        m["xT_pre"] = _fm(pre)
        m["cpack"] = cpk
        m["btile"] = bt
        in_maps.append(m)
    nc = build(NT)
    import os
    if os.environ.get("DEV_ONE"):
        res = run_bass_kernel_spmd(nc, in_maps[1:2], core_ids=[0], trace=trace)
        oT = np.asarray(res.results[0]["outT"])
        out = np.zeros((B, S, D), np.float32)
        out[0, HALF:] = oT.transpose(2, 1, 0).reshape(HALF, D)
        return out, res
    res = run_bass_kernel_spmd(nc, in_maps, core_ids=list(range(8)), trace=trace)
    out = np.empty((B, S, D), np.float32)
    for c in range(8):
        b, r = c // 2, c % 2
        oT = np.asarray(res.results[c]["outT"])
        out[b, r * HALF:(r + 1) * HALF] = oT.transpose(2, 1, 0).reshape(HALF, D)
    return out, res


def kernel(**inputs):
    out, _ = run(inputs)
    return out
```

```python
import math
import numpy as np
import concourse.bass as bass
import concourse.mybir as mybir
from concourse.bass_utils import run_bass_kernel_spmd

F32 = mybir.dt.float32
BF16 = mybir.dt.bfloat16
AF = mybir.ActivationFunctionType
ALU = mybir.AluOpType
AX = mybir.AxisListType

D = 2048
DC = 16
DFF = 5632
FC = 44
NH = 8
TT = 512
MASKV = -30000.0
LAM_INIT = 0.8 - 0.6 * math.exp(0.0)
NQ = 6

C_G1, C_GM, C_G2, C_GF = 0, 16, 32, 48
C_CB, C_LG, C_LB, C_HG = 64, 72, 80, 88
C_FLAG = 89
C_LAM = 96
C_FB = 352
C_CW = 368
C_ID = 616
C_N = 744


class Buf:
    __slots__ = ("w", "r", "name")

    def __init__(self, name=""):
        self.w = {}
        self.r = {}
        self.name = name


class Dyn:
    def __init__(self, fn):
        self.fn = fn


class Prog:
    ENG = ("pe", "act", "dve", "pool", "sp")

    def __init__(self, nc, esems, qsems):
        self.nc = nc
        self.streams = {e: [] for e in self.ENG}
        self.sem = esems
        self.cnt = {e: 0 for e in self.ENG}
        self.waited = {e: {} for e in self.ENG}
        self.qsems = qsems
        self.qidx = {q: 0 for q in qsems}
        self.qcum = {q: [0] * len(qsems[q]) for q in qsems}
        self.nwaits = 0
        self.segments = []

    def cut(self):
        self.segments.append(self.streams)
        self.streams = {e: [] for e in self.ENG}

    def _wait(self, eng, toks):
        for key, (h, v) in toks.items():
            if eng == "pe" and key == "pe":
                continue
            if self.waited[eng].get(key, 0) >= v:
                continue
            self.waited[eng][key] = v
            self.nwaits += 1
            self.streams[eng].append(lambda e, h=h, v=v: e.wait_ge(h, v))

    def deps(self, eng, reads, writes, pwrites=()):
        for b in reads:
            self._wait(eng, b.w)
        for b in writes:
            self._wait(eng, b.w)
            self._wait(eng, b.r)
        for b in pwrites:
            self._wait(eng, b.r)

    def _post(self, key, tok, reads, writes, pwrites):
        for b in reads:
            b.r[key] = tok
        for b in writes:
            b.w = {key: tok}
            b.r = {}
        for b in pwrites:
            b.w[key] = tok

    def op(self, eng, fn, reads=(), writes=(), pwrites=()):
        self.deps(eng, reads, writes, pwrites)
        self.cnt[eng] += 1
        h = self.sem[eng]
        tok = (h, self.cnt[eng])
        self.streams[eng].append(lambda e, fn=fn, h=h: fn(e).then_inc(h, 1))
        self._post(eng, tok, reads, writes, pwrites)

    def pe_group(self, mms, reads, writes, pwrites=()):
        eng = "pe"
        self.deps(eng, reads, writes, pwrites)
        self.cnt[eng] += 1
        h = self.sem[eng]
        tok = (h, self.cnt[eng])
        n = len(mms)
        for i, (o, l, r, st, sp) in enumerate(mms):
            if i == n - 1:
                self.streams[eng].append(
                    lambda e, o=o, l=l, r=r, st=st, sp=sp, h=h: e.matmul(o, lhsT=l, rhs=r, start=st, stop=sp).then_inc(h, 1))
            else:
                self.streams[eng].append(
                    lambda e, o=o, l=l, r=r, st=st, sp=sp: e.matmul(o, lhsT=l, rhs=r, start=st, stop=sp))
        self._post(eng, tok, reads, writes, pwrites)

    def dma(self, q, out_ap, in_ap, reads=(), writes=(), pwrites=()):
        i = self.qidx[q]
        self.qidx[q] = (i + 1) % len(self.qsems[q])
        h = self.qsems[q][i]
        key = (q, i)
        prev = self.qcum[q][i]
        if prev:
            self._wait(q, {key: (h, prev)})
        self.deps(q, reads, writes, pwrites)
        self.qcum[q][i] += 16
        tok = (h, self.qcum[q][i])
        self.streams[q].append(lambda e, o=out_ap, a=in_ap, h=h: e.dma_start(
            out=(o.fn() if isinstance(o, Dyn) else o), in_=(a.fn() if isinstance(a, Dyn) else a)).then_inc(h, 16))
        self._post(key, tok, reads, writes, pwrites)

    def all_tokens(self):
        t = {}
        for e in self.ENG:
            if self.cnt[e]:
                t[e] = (self.sem[e], self.cnt[e])
        for q in self.qsems:
            for i, h in enumerate(self.qsems[q]):
                if self.qcum[q][i]:
                    t[(q, i)] = (h, self.qcum[q][i])
        return t

    def barrier(self, engs=None):
        t = self.all_tokens()
        for e in (engs or self.ENG):
            self._wait(e, t)


def build(NT, ncores=8):
    NP = NT
    HALF = NT * TT
    NKEY = 2 * HALF
    NPB = NP * 4
    nc = bass.Bass("TRN2", target_bir_lowering=False, num_devices=ncores)

    def din(name, shape, dt=F32):
        return nc.dram_tensor(name, list(shape), dt, kind="ExternalInput").ap()

    xT_own = din("xT_own", [128, DC, HALF])
    w_g1 = din("w_g1", [D, DFF]); w_u1 = din("w_u1", [D, DFF]); w_d1 = din("w_d1", [DFF, D])
    w_g2 = din("w_g2", [D, DFF]); w_u2 = din("w_u2", [D, DFF]); w_d2 = din("w_d2", [DFF, D])
    w_in = din("w_in", [D, 9216])
    w_ab = din("w_ab", [1024, D]); w_cb = din("w_cb", [1024, D]); w_o = din("w_o", [D, D])
    cpack_d = din("cpack", [128, C_N])
    btile_d = din("btile", [NH, 128, 384])
    outT = nc.dram_tensor("outT", [128, DC, HALF], F32, kind="ExternalOutput").ap()
    KT_sh = nc.dram_tensor("KT_sh", [ncores, NH, 128, HALF], BF16, addr_space="Shared").ap()
    V_sh = nc.dram_tensor("V_sh", [ncores, HALF, 1024], BF16, addr_space="Shared").ap()
    u_sh = nc.dram_tensor("u_sh", [ncores, 128, 8, 32], BF16, addr_space="Shared").ap()
    KT_scr = nc.dram_tensor("KT_scr", [NH, 128, NKEY], BF16).ap()
    V_scr = nc.dram_tensor("V_scr", [NKEY, 1024], BF16).ap()
    x1_scr = nc.dram_tensor("x1_scr", [128, DC, HALF], F32).ap()
    u_scr = nc.dram_tensor("u_scr", [128, 8, HALF], BF16).ap()
    dyn = {}

    def set_pid(e):
        pid = e.partition_id()
        dyn["pid"] = pid
        dyn["oth"] = (pid // 2) * 2 + (1 - pid % 2)

    def slot(ap4, which, *idx):
        def fn():
            v = ap4[(bass.ds(dyn[which], 1),) + tuple(idx)]
            nd = len(v.shape)
            if nd == 3:
                return v.rearrange("a p f -> p (a f)")
            return v.rearrange("a p b f -> p (a b) f")
        return Dyn(fn)

    import contextlib
    es = contextlib.ExitStack()
    with es:
        def sb(name, shape, dt):
            return es.enter_context(nc.sbuf_tensor(name, list(shape), dt))

        xT = sb("xT", [128, DC, TT], F32)
        hT = sb("hT", [128, DC, TT], BF16)
        uT = sb("uT", [128, 8, 544], BF16)
        wbuf = sb("wbuf", [128, 2, 16 * 512], BF16)
        cp = sb("cp", [128, C_N], F32)
        bc = sb("bc", [128, 3, TT], F32)
        stg = sb("stg", [128, 2, TT], BF16)
        sqs = sb("sqs", [128, 2, TT], F32)
        onesD = sb("onesD", [128, 128], F32)
        onesC = sb("onesC", [128, 128], F32)
        onesH = sb("onesH", [128, 128], F32)
        onesB = sb("onesB", [128, 128], BF16)
        sm = sb("sm", [128, 16], F32)
        lt = sb("lt", [128, 64], F32)
        AR_W = 93 * 256
        arena = sb("arena", [128, AR_W], F32)
        ps = [es.enter_context(nc.psum_tensor(f"ps{i}", [128, TT], F32)) for i in range(8)]
        esems = {e: es.enter_context(nc.semaphore(f"s_{e}")) for e in Prog.ENG}
        qsems = {q: [es.enter_context(nc.semaphore(f"q_{q}{i}")) for i in range(NQ)] for q in ("sp", "pool")}
        P = Prog(nc, esems, qsems)

        def av(kib0, kib1, dt):
            a = arena[:, kib0 * 256:kib1 * 256]
            return a.bitcast(dt) if dt != F32 else a

        actT = av(0, 44, BF16)
        sgt = av(48, 52, BF16)
        ostg = av(56, 88, F32)
        qT = av(0, 8, BF16)
        sgA = av(8, 24, BF16)
        sgC = av(24, 40, BF16)
        KTb = av(40, 56, BF16)
        yT = av(40, 56, F32)
        Vb = av(56, 64, BF16)
        Pb = av(64, 68, BF16)
        BTb = av(68, 71, F32)
        fscr = av(71, 85, F32)
        Db = av(85, 93, BF16)

        def ch(ap2d, c, w=TT):
            return ap2d[:, c * w:(c + 1) * w]

        b_xT = [Buf(f"xT{c}") for c in range(DC)]
        b_hT = [Buf(f"hT{c}") for c in range(DC)]
        b_act = [Buf(f"act{c}") for c in range(FC)]
        b_sgt = [Buf() for _ in range(4)]
        b_uT = Buf("uT")
        b_w = [Buf("w0"), Buf("w1")]
        b_cp = Buf("cp")
        b_bc = [Buf(), Buf(), Buf()]
        b_stg = [Buf(), Buf()]
        b_sqs = [Buf(), Buf()]
        b_const = Buf("const")
        b_sm = Buf("sm")
        b_lt = Buf("lt")
        b_ps = [Buf(f"ps{i}") for i in range(8)]
        b_KT = [Buf(), Buf()]
        b_V = Buf("V")
        b_P = [Buf() for _ in range(4)]
        b_BT = [Buf(), Buf()]
        b_f = [Buf() for _ in range(7)]
        b_D = Buf("D")
        b_y = [Buf() for _ in range(8)]
        b_ostg = [Buf() for _ in range(DC)]
        b_x1s = Buf("x1s")
        b_us = Buf("us")
        b_ush = Buf("ush")
        b_KTs = Buf("KT_scr")
        b_Vs = Buf("V_scr")
        b_out = Buf("out")

        def cpc(col, n=1):
            return cp[:, col:col + n]

        P.dma("sp", cp[:], cpack_d, writes=[b_cp])
        P.op("pool", lambda e: e.memset(onesD[:], 1.0 / D), writes=[b_const])
        P.op("pool", lambda e: e.memset(onesC[:], 1.0 / 1024.0), writes=[b_const])
        P.op("pool", lambda e: e.memset(onesH[:], 1.0 / 128.0), writes=[b_const])
        P.op("pool", lambda e: e.memset(onesB[:], 1.0), writes=[b_const])
        P.op("dve", lambda e: e.tensor_tensor(out=lt[:], in0=cpc(C_LAM, 64), in1=cpc(C_LAM + 64, 64), op=ALU.mult),
             reads=[b_cp], writes=[b_lt])
        P.op("dve", lambda e: e.reduce_sum(out=sm[:, 2:3], in_=lt[:], axis=AX.X), reads=[b_lt], writes=[b_sm])
        P.op("dve", lambda e: e.tensor_tensor(out=lt[:], in0=cpc(C_LAM + 128, 64), in1=cpc(C_LAM + 192, 64), op=ALU.mult),
             reads=[b_cp, b_sm], writes=[b_lt])
        P.op("dve", lambda e: e.reduce_sum(out=sm[:, 3:4], in_=lt[:], axis=AX.X), reads=[b_lt], writes=[b_sm])
        P.op("act", lambda e: e.activation(out=sm[:, 4:6], in_=sm[:, 2:4], func=AF.Exp), reads=[b_sm], writes=[b_sm])
        P.op("dve", lambda e: e.tensor_tensor(out=sm[:, 6:7], in0=sm[:, 5:6], in1=sm[:, 4:5], op=ALU.subtract),
             reads=[b_sm], writes=[b_sm])
        P.op("dve", lambda e: e.tensor_scalar_add(out=sm[:, 0:1], in0=sm[:, 6:7], scalar1=-LAM_INIT),
             reads=[b_sm], writes=[b_sm])
        P.op("dve", lambda e: e.tensor_scalar_mul(out=sm[:, 1:2], in0=cpc(C_HG), scalar1=1.0 - LAM_INIT),
             reads=[b_sm, b_cp], writes=[b_sm])

        P.op("pool", lambda e: e.memset(sm[:, 8:9], 1e-6), writes=[b_sm])
        P.op("pool", lambda e: e.memset(sm[:, 9:10], 1e-5), writes=[b_sm])

        def rstd_from(src_ap, b_src, epscol, out_ap, b_o):
            P.op("act", lambda e: e.activation(out=out_ap, in_=src_ap, func=AF.Sqrt, bias=sm[:, epscol:epscol + 1]),
                 reads=[b_src, b_sm], writes=[b_o])
            P.op("dve", lambda e: e.reciprocal(out=out_ap, in_=out_ap), reads=[b_o], writes=[b_o])

        wplan = []
        wstate = {"issued": 0, "used": 0}

        def w_issue():
            i = wstate["issued"]
            if i >= len(wplan):
                return
            W, kc0, nkc, col0, ncols = wplan[i]
            s = i % 2
            dst = wbuf[:, s, 0:nkc * ncols].rearrange("p (k m) -> p k m", k=nkc)
            src = W[kc0 * 128:(kc0 + nkc) * 128, col0:col0 + ncols].rearrange("(k p) m -> p k m", p=128)
            P.dma("pool", dst, src, writes=[b_w[s]])
            wstate["issued"] += 1

        def w_get(spec):
            i = wstate["used"]
            assert wplan[i] == spec, (i, wplan[i][1:], spec[1:])
            while wstate["issued"] < min(i + 2, len(wplan)):
                w_issue()
            wstate["used"] += 1
            s = i % 2
            return wbuf[:, s, :], b_w[s]

        def plan_ffn(wg, wu, wd):
            for i in range(DFF // 512):
                wplan.append((wg, 0, 16, i * 512, 512))
                wplan.append((wu, 0, 16, i * 512, 512))
            for j in range(4):
                for (k0, nk) in ((0, 16), (16, 16), (32, 12)):
                    wplan.append((wd, k0, nk, j * 512, 512))

        def plan_lin(W, K, cols):
            for c0 in cols:
                wplan.append((W, 0, K // 128, c0, 512))

        KV_COLS = [1024, 1536, 2048, 2560]
        GLU_COLS = [3072, 4096, 3584, 4608]
        for t in range(NT):
            plan_ffn(w_g1, w_u1, w_d1)
            plan_lin(w_in, D, KV_COLS + GLU_COLS)
        for t in range(NT):
            plan_lin(w_in, D, [0, 512] + [5120 + 512 * i for i in range(8)])
            plan_lin(w_ab, 1024, [0, 512, 1024, 1536])
            plan_lin(w_cb, 1024, [0, 512, 1024, 1536])
            plan_lin(w_o, D, [0, 512, 1024, 1536])
            plan_ffn(w_g2, w_u2, w_d2)

        def lin_group(W, K, col0, in_aps, in_bufs, banks, evac, tokmajor=False, kgroups=None):
            kgroups = kgroups or [(0, K // 128)]
            ng = len(kgroups)
            for gi, (k0, nk) in enumerate(kgroups):
                wap, wb = w_get((W, k0, nk, col0, 512))
                for m in range(4):
                    mms = []
                    for kk in range(nk):
                        st = (gi == 0 and kk == 0)
                        sp = (gi == ng - 1 and kk == nk - 1)
                        if tokmajor:
                            l = in_aps[k0 + kk][:, m * 128:(m + 1) * 128]
                            r = wap[:, kk * 512:(kk + 1) * 512]
                        else:
                            l = wap[:, kk * 512 + m * 128: kk * 512 + (m + 1) * 128]
                            r = in_aps[k0 + kk]
                        mms.append((ps[banks[m]][:], l, r, st, sp))
                    P.pe_group(mms, reads=[wb] + [in_bufs[k0 + kk] for kk in range(nk)],
                               writes=[b_ps[banks[m]]] if gi == 0 else [],
                               pwrites=[] if gi == 0 else [b_ps[banks[m]]])
            for m in range(4):
                evac(m, banks[m])

        def rmsnorm_stats(gdummy=None):
            for c in range(DC):
                s = c % 2
                P.op("act", lambda e, c=c, s=s: e.activation(out=sqs[:, s, :], in_=xT[:, c, :], func=AF.Square),
                     reads=[b_xT[c]], writes=[b_sqs[s]])
                P.pe_group([(ps[0][:], onesD[:], sqs[:, s, :], c == 0, c == DC - 1)],
                           reads=[b_sqs[s], b_const], writes=[b_ps[0]] if c == 0 else [],
                           pwrites=[] if c == 0 else [b_ps[0]])
            rstd_from(ps[0][:], b_ps[0], 8, bc[:, 0, :], b_bc[0])

        def norm_to_hT(gcol):
            rmsnorm_stats()
            for c in range(DC):
                P.op("dve", lambda e, c=c: e.scalar_tensor_tensor(out=hT[:, c, :], in0=xT[:, c, :], scalar=cpc(gcol + c),
                                                                 in1=bc[:, 0, :], op0=ALU.mult, op1=ALU.mult),
                     reads=[b_xT[c], b_bc[0], b_cp], writes=[b_hT[c]])

        hT_aps = [hT[:, c, :] for c in range(DC)]
        act_aps = [ch(actT, f) for f in range(FC)]

        def ffn(wg, wu, wd, gcol):
            norm_to_hT(gcol)
            for i in range(DFF // 512):
                def ev_gate(m, bank):
                    P.op("act", lambda e, m=m, bank=bank: e.activation(out=ch(sgt, m), in_=ps[bank][:], func=AF.Silu),
                         reads=[b_ps[bank]], writes=[b_sgt[m]])

                def ev_up(m, bank, i=i):
                    f = 4 * i + m
                    P.op("dve", lambda e, m=m, bank=bank, f=f: e.tensor_tensor(out=act_aps[f], in0=ps[bank][:],
                                                                             in1=ch(sgt, m), op=ALU.mult),
                         reads=[b_ps[bank], b_sgt[m]], writes=[b_act[f]])
                lin_group(wg, D, i * 512, hT_aps, b_hT, [0, 1, 2, 3], ev_gate)
                lin_group(wu, D, i * 512, hT_aps, b_hT, [4, 5, 6, 7], ev_up)
            for j in range(4):
                def ev_dn(m, bank, j=j):
                    c = 4 * j + m
                    P.op("dve", lambda e, c=c, bank=bank: e.scalar_tensor_tensor(out=xT[:, c, :], in0=ps[bank][:], scalar=0.5,
                                                                               in1=xT[:, c, :], op0=ALU.mult, op1=ALU.add),
                         reads=[b_ps[bank], b_xT[c]], writes=[b_xT[c]])
                banks = [0, 1, 2, 3] if j % 2 == 0 else [4, 5, 6, 7]
                lin_group(wd, DFF, j * 512, act_aps, b_act, banks, ev_dn, kgroups=[(0, 16), (16, 16), (32, 12)])

        def load_x(src, t):
            P.dma("sp", xT[:], src[:, :, t * TT:(t + 1) * TT], writes=b_xT)

        bank_rr = {"i": 0}

        def next_banks():
            bank_rr["i"] ^= 1
            return [0, 1, 2, 3] if bank_rr["i"] else [4, 5, 6, 7]

        def proj_kv(pos):
            for g in range(2):
                def ev_k(m, bank, g=g):
                    h = 4 * g + m
                    s = h % 2
                    P.op("act", lambda e, s=s, bank=bank: e.activation(out=stg[:, s, :], in_=ps[bank][:], func=AF.Copy),
                         reads=[b_ps[bank]], writes=[b_stg[s]])
                    P.dma("sp", KT_scr[h, :, pos:pos + TT], stg[:, s, :], reads=[b_stg[s]], pwrites=[b_KTs])
                lin_group(w_in, D, 1024 + g * 512, hT_aps, b_hT, next_banks(), ev_k)
            for g in range(2):
                def ev_v(m, bank, g=g):
                    s = m % 2
                    P.op("dve", lambda e, s=s, bank=bank: e.tensor_copy(out=stg[:, s, :], in_=ps[bank][:]),
                         reads=[b_ps[bank]], writes=[b_stg[s]])
                    P.dma("sp", V_scr[pos + m * 128:pos + (m + 1) * 128, g * 512:(g + 1) * 512], stg[:, s, :],
                          reads=[b_stg[s]], pwrites=[b_Vs])
                lin_group(w_in, D, 2048 + g * 512, hT_aps, b_hT, next_banks(), ev_v, tokmajor=True)

        def proj_glu():
            for g in range(2):
                ba = next_banks()
                bb = next_banks()

                def ev_a(m, bank):
                    pass

                def ev_b(m, bank, g=g, ba=ba):
                    c = 4 * g + m
                    s = m % 2
                    P.op("act", lambda e, bank=bank, s=s: e.activation(out=sqs[:, s, :], in_=ps[bank][:], func=AF.Sigmoid),
                         reads=[b_ps[bank]], writes=[b_sqs[s]])
                    P.op("dve", lambda e, c=c, s=s, m=m: e.tensor_tensor(out=uT[:, c, 30:542], in0=ps[ba[m]][:],
                                                                        in1=sqs[:, s, :], op=ALU.mult),
                         reads=[b_ps[ba[m]], b_sqs[s]], writes=[b_uT])
                lin_group(w_in, D, 3072 + g * 512, hT_aps, b_hT, ba, ev_a)
                lin_group(w_in, D, 4096 + g * 512, hT_aps, b_hT, bb, ev_b)

        import os
        STOP = int(os.environ.get("DEV_STOP", "0"))

        class _Stop(Exception):
            pass

        def checkpoint(k):
            if STOP != k:
                return
            P.barrier()
            for c in range(DC):
                P.op("dve", lambda e, c=c: e.tensor_copy(out=ch(ostg, c), in_=xT[:, c, :]),
                     reads=[b_xT[c]], writes=[b_ostg[c]])
            P.dma("sp", outT[:, :, 0:TT], ostg.rearrange("p (c t) -> p c t", c=DC), reads=b_ostg, pwrites=[b_out])
            raise _Stop()

        def emit_all():
            P.streams["pool"].append(set_pid)
            P.op("pool", lambda e: e.memset(uT[:], 0.0), writes=[b_uT])
            for t in range(NT):
                load_x(xT_own, t)
                ffn(w_g1, w_u1, w_d1, C_G1)
                P.dma("sp", x1_scr[:, :, t * TT:(t + 1) * TT], xT[:], reads=b_xT, pwrites=[b_x1s])
                norm_to_hT(C_GM)
                proj_kv(HALF + t * TT)
                proj_glu()
                P.dma("sp", u_scr[:, :, t * TT:(t + 1) * TT], uT[:, :, 30:542], reads=[b_uT], pwrites=[b_us])
                if t == NT - 1:
                    P.dma("pool", slot(u_sh, "pid", slice(None), slice(None), slice(0, 30)), uT[:, :, 512:542],
                          reads=[b_uT], pwrites=[b_ush])
            P.dma("pool", Dyn(lambda: KT_sh[bass.ds(dyn["pid"], 1), :, :, :].rearrange("a h p f -> (a h p) f")),
                  KT_scr[:, :, HALF:NKEY].rearrange("h p f -> (h p) f"), reads=[b_KTs], pwrites=[b_ush])
            P.dma("pool", Dyn(lambda: V_sh[bass.ds(dyn["pid"], 1), :, :].rearrange("a t c -> (a t) c")),
                  V_scr[HALF:NKEY, :], reads=[b_Vs], pwrites=[b_ush])
            P.barrier()
            P.cut()
            P.streams["pool"].append(set_pid)
            P.dma("pool", KT_scr[:, :, 0:HALF].rearrange("h p f -> (h p) f"),
                  Dyn(lambda: KT_sh[bass.ds(dyn["oth"], 1), :, :, :].rearrange("a h p f -> (a h p) f")),
                  reads=[b_ush], pwrites=[b_KTs])
            P.dma("pool", V_scr[0:HALF, :], Dyn(lambda: V_sh[bass.ds(dyn["oth"], 1), :, :].rearrange("a t c -> (a t) c")),
                  reads=[b_ush], pwrites=[b_Vs])

            q_aps = [ch(qT, h) for h in range(NH)]
            b_q = b_act[0:8]
            sgA_aps = [ch(sgA, c) for c in range(DC)]
            b_sgA = b_act[8:24]
            sgC_aps = [ch(sgC, c) for c in range(DC)]
            b_sgC = b_act[24:40]
            on_aps = [hT[:, h, :] for h in range(8)]
            b_on = b_hT[0:8]
            c_aps = [hT[:, 8 + c, :] for c in range(8)]
            b_c = b_hT[8:16]
            f_aps = [ch(fscr, i) for i in range(7)]
            y_aps = [ch(yT, i) for i in range(8)]

            for t in range(NT):
                pos = HALF + t * TT
                G0 = pos // 128
                P.dma("sp", xT[:], x1_scr[:, :, t * TT:(t + 1) * TT], reads=[b_x1s], writes=b_xT)
                P.dma("sp", uT[:, :, 30:542], u_scr[:, :, t * TT:(t + 1) * TT], reads=[b_us], writes=[b_uT])
                if t == 0:
                    P.dma("pool", uT[:, :, 0:30], slot(u_sh, "oth", slice(None), slice(None), slice(0, 30)),
                          reads=[b_ush], pwrites=[b_uT])
                    P.op("dve", lambda e: e.tensor_scalar_mul(out=uT[:, :, 0:30], in0=uT[:, :, 0:30], scalar1=cpc(C_FLAG)),
                         reads=[b_uT, b_cp], writes=[b_uT])
                else:
                    P.dma("sp", uT[:, :, 0:30], u_scr[:, :, t * TT - 30:t * TT], reads=[b_us], pwrites=[b_uT])
                norm_to_hT(C_GM)
                for g in range(2):
                    def ev_q(m, bank, g=g):
                        h = 4 * g + m
                        P.op("act", lambda e, h=h, bank=bank: e.activation(out=q_aps[h], in_=ps[bank][:], func=AF.Copy),
                             reads=[b_ps[bank]], writes=[b_q[h]])
                    lin_group(w_in, D, g * 512, hT_aps, b_hT, next_banks(), ev_q)
                for i in range(8):
                    def ev_s(m, bank, i=i):
                        c = (4 * i + m) % 16
                        dst, bb_ = (sgA_aps[c], b_sgA[c]) if i < 4 else (sgC_aps[c], b_sgC[c])
                        P.op("act", lambda e, dst=dst, bank=bank: e.activation(out=dst, in_=ps[bank][:], func=AF.Sigmoid),
                             reads=[b_ps[bank]], writes=[bb_])
                    lin_group(w_in, D, 5120 + 512 * i, hT_aps, b_hT, next_banks(), ev_s)
                P.barrier()
                nkb = G0 + 4
                for h in range(NH):
                    kb_ = h % 2
                    KTh = KTb[:, kb_ * 4096: kb_ * 4096 + nkb * 128]
                    P.dma("sp", KTh, KT_scr[h, :, 0:nkb * 128], reads=[b_KTs], writes=[b_KT[kb_]])
                    P.dma("sp", Vb[:, 0:nkb * 128].rearrange("p (b d) -> p b d", d=128),
                          V_scr[0:nkb * 128, h * 128:(h + 1) * 128].rearrange("(b p) d -> p b d", p=128),
                          reads=[b_Vs], writes=[b_V])
                    bt_ = h % 2
                    P.dma("sp", BTb[:, bt_ * 384:(bt_ + 1) * 384], btile_d[h], writes=[b_BT[bt_]])

                    def s_stage(kb, sset):
                        qi0 = max(0, kb - G0)
                        qlo = qi0 * 128
                        s1, s2 = (0, 1) if sset == 0 else (2, 3)
                        ks = slice(kb * 128, (kb + 1) * 128)
                        P.pe_group([(ps[s1][:, qlo:TT], KTb[0:64, kb_ * 4096 + kb * 128: kb_ * 4096 + (kb + 1) * 128],
                                     qT[0:64, h * TT + qlo:(h + 1) * TT], True, True)],
                                   reads=[b_KT[kb_], b_q[h]], writes=[b_ps[s1]])
                        P.pe_group([(ps[s2][:, qlo:TT], KTb[64:128, kb_ * 4096 + kb * 128: kb_ * 4096 + (kb + 1) * 128],
                                     qT[64:128, h * TT + qlo:(h + 1) * TT], True, True)],
                                   reads=[b_KT[kb_], b_q[h]], writes=[b_ps[s2]])
                        fbcol = C_FB + (8 + h if kb < NPB else h)
                        for which, bank in ((0, s1), (1, s2)):
                            pi = sset * 2 + which
                            pap = ch(Pb, pi)
                            far_lo = None
                            first = True
                            for qi in range(qi0, 4):
                                dblk = G0 + qi - kb
                                if dblk >= 2:
                                    far_lo = qi * 128
                                    break
                                typ = 0 if dblk == 0 else (2 if (kb == NPB - 1) else 1)
                                tmp_i = 5 if which == 0 else 6
                                tsl = f_aps[tmp_i][:, qi * 128:(qi + 1) * 128]
                                btap = BTb[:, bt_ * 384 + typ * 128: bt_ * 384 + (typ + 1) * 128]
                                P.op("dve", lambda e, tsl=tsl, bank=bank, qi=qi, btap=btap: e.scalar_tensor_tensor(
                                    out=tsl, in0=ps[bank][:, qi * 128:(qi + 1) * 128], scalar=0.125,
                                    in1=btap, op0=ALU.mult, op1=ALU.add),
                                    reads=[b_ps[bank], b_BT[bt_]], writes=[b_f[tmp_i]])
                                P.op("act", lambda e, tsl=tsl, pap=pap, qi=qi: e.activation(
                                    out=pap[:, qi * 128:(qi + 1) * 128], in_=tsl, func=AF.Exp),
                                    reads=[b_f[tmp_i]], writes=[b_P[pi]] if first else [], pwrites=[] if first else [b_P[pi]])
                                first = False
                            if far_lo is not None:
                                P.op("act", lambda e, pap=pap, bank=bank, far_lo=far_lo, fbcol=fbcol: e.activation(
                                    out=pap[:, far_lo:TT], in_=ps[bank][:, far_lo:TT], func=AF.Exp,
                                    bias=cpc(fbcol), scale=0.125),
                                    reads=[b_ps[bank], b_cp], writes=[b_P[pi]] if first else [], pwrites=[] if first else [b_P[pi]])
                        return qlo

                    def av_stage(kb, sset, qlo, first, last):
                        for which in (0, 1):
                            pi = sset * 2 + which
                            pap = ch(Pb, pi)[:, qlo:TT]
                            ob = 4 + which
                            lb = 6 + which
                            P.pe_group([(ps[ob][:, qlo:TT], Vb[:, kb * 128:(kb + 1) * 128], pap, first, last)],
                                       reads=[b_V, b_P[pi]], writes=[b_ps[ob]] if first else [],
                                       pwrites=[] if first else [b_ps[ob]])
                            P.pe_group([(ps[lb][:, qlo:TT], onesB[:], pap, first, last)],
                                       reads=[b_P[pi], b_const], writes=[b_ps[lb]] if first else [],
                                       pwrites=[] if first else [b_ps[lb]])

                    prev = None
                    for kb in range(nkb):
                        sset = kb % 2
                        qlo = s_stage(kb, sset)
                        if prev is not None:
                            av_stage(prev[0], prev[1], prev[2], prev[0] == 0, False)
                        prev = (kb, sset, qlo)
                    av_stage(prev[0], prev[1], prev[2], prev[0] == 0, True)
                    P.op("dve", lambda e: e.reciprocal(out=f_aps[0], in_=ps[6][:]), reads=[b_ps[6]], writes=[b_f[0]])
                    P.op("dve", lambda e: e.reciprocal(out=f_aps[1], in_=ps[7][:]), reads=[b_ps[7]], writes=[b_f[1]])
                    P.op("dve", lambda e: e.tensor_tensor(out=f_aps[2], in0=ps[4][:], in1=f_aps[0], op=ALU.mult),
                         reads=[b_ps[4], b_f[0]], writes=[b_f[2]])
                    P.op("dve", lambda e: e.tensor_tensor(out=f_aps[3], in0=ps[5][:], in1=f_aps[1], op=ALU.mult),
                         reads=[b_ps[5], b_f[1]], writes=[b_f[3]])
                    P.op("dve", lambda e: e.scalar_tensor_tensor(out=f_aps[4], in0=f_aps[3], scalar=sm[:, 0:1], in1=f_aps[2],
                                                                 op0=ALU.mult, op1=ALU.add),
                         reads=[b_f[2], b_f[3], b_sm], writes=[b_f[4]])
                    P.op("act", lambda e: e.activation(out=f_aps[0], in_=f_aps[4], func=AF.Square),
                         reads=[b_f[4]], writes=[b_f[0]])
                    P.pe_group([(ps[0][:], onesH[:], f_aps[0], True, True)], reads=[b_f[0], b_const], writes=[b_ps[0]])
                    rstd_from(ps[0][:], b_ps[0], 9, bc[:, 1, :], b_bc[1])
                    P.op("dve", lambda e, h=h: e.scalar_tensor_tensor(out=on_aps[h], in0=f_aps[4], scalar=sm[:, 1:2],
                                                                     in1=bc[:, 1, :], op0=ALU.mult, op1=ALU.mult),
                         reads=[b_f[4], b_bc[1], b_sm], writes=[b_on[h]])
                checkpoint(5)
                P.barrier()
                for j in range(4):
                    def ev_ab(m, bank, j=j):
                        c = 4 * j + m
                        P.op("dve", lambda e, c=c, bank=bank: e.tensor_tensor(out=sgA_aps[c], in0=ps[bank][:], in1=sgA_aps[c],
                                                                            op=ALU.mult),
                             reads=[b_ps[bank], b_sgA[c]], writes=[b_sgA[c]])
                    lin_group(w_ab, 1024, j * 512, on_aps, b_on, next_banks(), ev_ab)
                for c in range(8):
                    for j in range(31):
                        P.op("dve", lambda e, c=c, j=j: e.tensor_scalar_mul(out=Db[:, j * 128:(j + 1) * 128],
                                                                           in0=cpc(C_ID, 128), scalar1=cpc(C_CW + c * 31 + j)),
                             reads=[b_cp], writes=[b_D] if j == 0 else [], pwrites=[] if j == 0 else [b_D])
                    bank = 0 if c % 2 == 0 else 1
                    mms = [(ps[bank][:], Db[:, j * 128:(j + 1) * 128], uT[:, c, j:j + TT], j == 0, j == 30) for j in range(31)]
                    P.pe_group(mms, reads=[b_D, b_uT], writes=[b_ps[bank]])
                    P.op("act", lambda e, c=c, bank=bank: e.activation(out=y_aps[c], in_=ps[bank][:], func=AF.Identity,
                                                                      bias=cpc(C_CB + c)),
                         reads=[b_ps[bank], b_cp], writes=[b_y[c]])
                    P.pe_group([(ps[2][:], onesC[:], y_aps[c], c == 0, c == 7)], reads=[b_y[c], b_const],
                               writes=[b_ps[2]] if c == 0 else [], pwrites=[] if c == 0 else [b_ps[2]])
                    s = c % 2
                    P.op("act", lambda e, c=c, s=s: e.activation(out=sqs[:, s, :], in_=y_aps[c], func=AF.Square),
                         reads=[b_y[c]], writes=[b_sqs[s]])
                    P.pe_group([(ps[3][:], onesC[:], sqs[:, s, :], c == 0, c == 7)], reads=[b_sqs[s], b_const],
                               writes=[b_ps[3]] if c == 0 else [], pwrites=[] if c == 0 else [b_ps[3]])
                P.op("dve", lambda e: e.tensor_copy(out=bc[:, 2, :], in_=ps[2][:]), reads=[b_ps[2]], writes=[b_bc[2]])
                P.op("dve", lambda e: e.tensor_tensor(out=bc[:, 1, :], in0=bc[:, 2, :], in1=bc[:, 2, :], op=ALU.mult),
                     reads=[b_bc[2]], writes=[b_bc[1]])
                P.op("dve", lambda e: e.tensor_tensor(out=bc[:, 1, :], in0=ps[3][:], in1=bc[:, 1, :], op=ALU.subtract),
                     reads=[b_ps[3], b_bc[1]], writes=[b_bc[1]])
                rstd_from(bc[:, 1, :], b_bc[1], 9, bc[:, 1, :], b_bc[1])
                for c in range(8):
                    P.op("dve", lambda e, c=c: e.tensor_tensor(out=y_aps[c], in0=y_aps[c], in1=bc[:, 2, :], op=ALU.subtract),
                         reads=[b_y[c], b_bc[2]], writes=[b_y[c]])
                    P.op("dve", lambda e, c=c: e.tensor_tensor(out=y_aps[c], in0=y_aps[c], in1=bc[:, 1, :], op=ALU.mult),
                         reads=[b_y[c], b_bc[1]], writes=[b_y[c]])
                    P.op("act", lambda e, c=c: e.activation(out=c_aps[c], in_=y_aps[c], func=AF.Silu,
                                                           bias=cpc(C_LB + c), scale=cpc(C_LG + c)),
                         reads=[b_y[c], b_cp], writes=[b_c[c]])
                for j in range(4):
                    def ev_cb(m, bank, j=j):
                        c = 4 * j + m
                        P.op("dve", lambda e, c=c, bank=bank: e.tensor_tensor(out=sgC_aps[c], in0=ps[bank][:], in1=sgC_aps[c],
                                                                            op=ALU.mult),
                             reads=[b_ps[bank], b_sgC[c]], writes=[b_sgC[c]])
                        P.op("dve", lambda e, c=c: e.tensor_tensor(out=sgA_aps[c], in0=sgA_aps[c], in1=sgC_aps[c], op=ALU.add),
                             reads=[b_sgA[c], b_sgC[c]], writes=[b_sgA[c]])
                    lin_group(w_cb, 1024, j * 512, c_aps, b_c, next_banks(), ev_cb)
                for j in range(4):
                    def ev_o(m, bank, j=j):
                        c = 4 * j + m
                        P.op("dve", lambda e, c=c, bank=bank: e.tensor_tensor(out=xT[:, c, :], in0=ps[bank][:], in1=xT[:, c, :],
                                                                            op=ALU.add),
                             reads=[b_ps[bank], b_xT[c]], writes=[b_xT[c]])
                    lin_group(w_o, D, j * 512, sgA_aps, b_sgA, next_banks(), ev_o)
                P.barrier()
                checkpoint(6)
                ffn(w_g2, w_u2, w_d2, C_G2)
                rmsnorm_stats()
                for c in range(DC):
                    P.op("dve", lambda e, c=c: e.scalar_tensor_tensor(out=ch(ostg, c), in0=xT[:, c, :], scalar=cpc(C_GF + c),
                                                                     in1=bc[:, 0, :], op0=ALU.mult, op1=ALU.mult),
                         reads=[b_xT[c], b_bc[0], b_cp], writes=[b_ostg[c]])
                P.dma("sp", outT[:, :, t * TT:(t + 1) * TT], ostg.rearrange("p (c t) -> p c t", c=DC),
                      reads=b_ostg, pwrites=[b_out])


        try:
            emit_all()
        except _Stop:
            wstate["used"] = len(wplan)

        assert wstate["used"] == len(wplan), (wstate, len(wplan))
        P.barrier(["sp"])

        P.cut()
        for si, seg in enumerate(P.segments):
            if si > 0:
                nc.all_core_barrier()
            with nc.Block() as block:
                @block.tensor
                def _(e, seg=seg):
                    for f in seg["pe"]:
                        f(e)

                @block.scalar
                def _(e, seg=seg):
                    for f in seg["act"]:
                        f(e)

                @block.vector
                def _(e, seg=seg):
                    for f in seg["dve"]:
                        f(e)

                @block.gpsimd
                def _(e, seg=seg):
                    for f in seg["pool"]:
                        f(e)

                @block.sync
                def _(e, seg=seg):
                    for f in seg["sp"]:
                        f(e)
    return nc


def _t5_bucket(n):
    n = np.asarray(n, dtype=np.int64)
    nf = np.maximum(n, 1).astype(np.float32)
    large = 16 + (np.log(nf / np.float32(16)) / np.float32(math.log(8.0)) * np.float32(16)).astype(np.int32)
    large = np.minimum(large, 31)
    return np.where(n < 16, n, large).astype(np.int64)


def _fm(x2d):
    T, F = x2d.shape
    return np.ascontiguousarray(x2d.reshape(T, F // 128, 128).transpose(2, 1, 0))


def _pc(v):
    return np.ascontiguousarray(np.asarray(v, np.float32).reshape(-1, 128).T)


def run(inputs, trace=False):
    x = np.asarray(inputs["x"], np.float32)
    B, S, _ = x.shape
    HALF = S // 2
    NT = HALF // TT
    assert HALF % TT == 0 and B == 4
    f32 = lambda k: np.ascontiguousarray(np.asarray(inputs[k], np.float32))
    table = f32("rel_bias_table")
    ii = np.arange(128)[:, None]
    jj = np.arange(128)[None, :]
    n_diag = jj - ii
    n_near = 128 + jj - ii
    idx_diag = _t5_bucket(np.maximum(n_diag, 0))
    idx_near = _t5_bucket(n_near)
    convw = f32("conv_dw_w")[0]
    base = np.zeros((128, C_N), np.float32)
    base[:, C_G1:C_G1 + 16] = _pc(f32("ffn1_norm_g")[0])
    base[:, C_GM:C_GM + 16] = _pc(f32("mix_norm_g")[0])
    base[:, C_G2:C_G2 + 16] = _pc(f32("ffn2_norm_g")[0])
    base[:, C_GF:C_GF + 16] = _pc(f32("final_norm_g"))
    base[:, C_CB:C_CB + 8] = _pc(f32("conv_dw_b")[0])
    base[:, C_LG:C_LG + 8] = _pc(f32("conv_ln_g")[0])
    base[:, C_LB:C_LB + 8] = _pc(f32("conv_ln_b")[0])
    base[:, C_HG] = f32("attn_head_norm_g")[0]
    for i, k in enumerate(("lambda_q1", "lambda_k1", "lambda_q2", "lambda_k2")):
        base[:, C_LAM + 64 * i:C_LAM + 64 * (i + 1)] = f32(k)[0][None, :]
    cw = convw.reshape(31, 8, 128).transpose(2, 1, 0).reshape(128, 8 * 31)
    base[:, C_CW:C_CW + 248] = cw
    base[:, C_ID:C_ID + 128] = np.eye(128, dtype=np.float32)
    shared = {
        "w_g1": f32("ffn1_w_gate")[0], "w_u1": f32("ffn1_w_up")[0], "w_d1": f32("ffn1_w_down")[0],
        "w_g2": f32("ffn2_w_gate")[0], "w_u2": f32("ffn2_w_up")[0], "w_d2": f32("ffn2_w_down")[0],
        "w_in": f32("w_in")[0], "w_ab": f32("w_attn_branch")[0], "w_cb": f32("w_conv_branch")[0],
        "w_o": f32("w_out")[0],
    }
    in_maps = []
    for c in range(8):
        b, r = c // 2, c % 2
        cpk = base.copy()
        bt = np.empty((NH, 128, 384), np.float32)
        for h in range(NH):
            tb = table[:, h]
            bt[h, :, 0:128] = np.where(n_diag >= 0, tb[idx_diag], np.float32(MASKV))
            near = tb[idx_near]
            bt[h, :, 128:256] = near
            bt[h, :, 256:384] = near if r == 1 else np.float32(MASKV)
            cpk[:, C_FB + h] = tb[31]
            cpk[:, C_FB + 8 + h] = tb[31] if r == 1 else np.float32(MASKV)
        cpk[:, C_FLAG] = 1.0 if r == 1 else 0.0
        own = x[b, r * HALF:(r + 1) * HALF]
        m = dict(shared)
        m["xT_own"] = _fm(own)
        m["cpack"] = cpk
        m["btile"] = bt
        in_maps.append(m)
    nc = build(NT)
    import os
    if os.environ.get("DEV_ONE"):
        res = run_bass_kernel_spmd(nc, in_maps[1:2], core_ids=[0], trace=trace)
        oT = np.asarray(res.results[0]["outT"])
        out = np.zeros((B, S, D), np.float32)
        out[0, HALF:] = oT.transpose(2, 1, 0).reshape(HALF, D)
        return out, res
    res = run_bass_kernel_spmd(nc, in_maps, core_ids=list(range(8)), trace=trace)
    out = np.empty((B, S, D), np.float32)
    for c in range(8):
        b, r = c // 2, c % 2
        oT = np.asarray(res.results[c]["outT"])
        out[b, r * HALF:(r + 1) * HALF] = oT.transpose(2, 1, 0).reshape(HALF, D)
    return out, res


def kernel(**inputs):
    out, _ = run(inputs)
    return out
```
